# Optimizing a Trainium2 kernel written in Bass

```python
import jax, jax.numpy as jnp
from jax import lax
import numpy as np

D_MODEL = 1024
BATCH = 8
SEQ = 2048
DEPTH = 4

N_MIXERS = 2
EXPAND = 2
D_INNER = EXPAND * D_MODEL
EPS = 1e-6

SGU_CHUNK = 128
SGU_GROUPS = 8
SGU_GROUP_DIM = D_INNER // SGU_GROUPS
A_IN = 3 * D_INNER

GLA_HEADS = 4
GLA_DK_TOTAL = D_MODEL // 2
GLA_DK = GLA_DK_TOTAL // GLA_HEADS
GLA_DV = D_INNER // GLA_HEADS
GLA_GATE_RANK = 16
GLA_GATE_TAU = 16.0
GLA_CHUNK = 64
B_SPLITS = [GLA_DK_TOTAL, 2 * GLA_DK_TOTAL, 2 * GLA_DK_TOTAL + D_INNER, 2 * GLA_DK_TOTAL + 2 * D_INNER]
B_IN = 2 * GLA_DK_TOTAL + 2 * D_INNER + GLA_GATE_RANK

N_A = (DEPTH + 1) // 2
N_B = DEPTH // 2

kernel_name = "hybrid_sgu_gla_adaln_trunk"


def rmsnorm(x, g):
    xf = x.astype(jnp.float32)
    y = xf * lax.rsqrt(jnp.mean(xf * xf, axis=-1, keepdims=True) + EPS)
    return (y * g.astype(jnp.float32)).astype(x.dtype)


def sgu_branch(h, w_in, w_s, b_s, g_v, w_out):
    bsz, seq, _ = h.shape
    u, v, z = jnp.split(h @ w_in, 3, axis=-1)
    u = jax.nn.gelu(u)
    v = rmsnorm(jax.nn.gelu(v), g_v)
    n_c = seq // SGU_CHUNK
    v = v.reshape(bsz, n_c, SGU_CHUNK, SGU_GROUPS, SGU_GROUP_DIM)
    causal = jnp.tril(jnp.ones((SGU_CHUNK, SGU_CHUNK), dtype=bool))
    w = jnp.where(causal[None], w_s, 0)
    mixed = jnp.einsum('gts,bcsgd->bctgd', w, v) + b_s.T[None, None, :, :, None]
    y = u * mixed.reshape(bsz, seq, D_INNER) * jax.nn.silu(z)
    return y @ w_out


def gla_branch(h, w_in, w_gate_up, b_gate, g_o, w_out):
    f32 = jnp.float32
    bsz, seq, _ = h.shape
    q, k, v, z, a_lr = jnp.split(h @ w_in, B_SPLITS, axis=-1)
    log_a = jax.nn.log_sigmoid((a_lr @ w_gate_up + b_gate).astype(f32)) / GLA_GATE_TAU
    L = GLA_CHUNK
    n_c = seq // L

    def heads(t, d):
        return t.astype(f32).reshape(bsz, n_c, L, GLA_HEADS, d).transpose(0, 1, 3, 2, 4)

    q = heads(q, GLA_DK) * (GLA_DK ** -0.5)
    k = heads(k, GLA_DK)
    v = heads(v, GLA_DV)
    b = jnp.cumsum(heads(log_a, GLA_DK), axis=3)
    b_last = b[:, :, :, -1:, :]
    q_dec = q * jnp.exp(b)
    k_inv = k * jnp.exp(-b)
    k_state = k * jnp.exp(b_last - b)
    causal = jnp.tril(jnp.ones((L, L), dtype=bool))
    attn = jnp.where(causal, jnp.einsum('bchtk,bchsk->bchts', q_dec, k_inv), 0.0)
    o_intra = jnp.einsum('bchts,bchsv->bchtv', attn, v)
    decay = jnp.exp(b_last[:, :, :, 0, :])

    def step(state, inp):
        q_c, ks_c, v_c, d_c = inp
        o = jnp.einsum('bhtk,bhkv->bhtv', q_c, state)
        state = d_c[..., None] * state + jnp.einsum('bhsk,bhsv->bhkv', ks_c, v_c)
        return state, o

    s0 = jnp.zeros((bsz, GLA_HEADS, GLA_DK, GLA_DV), f32)
    _, o_inter = lax.scan(step, s0, (q_dec.swapaxes(0, 1), k_state.swapaxes(0, 1),
                                     v.swapaxes(0, 1), decay.swapaxes(0, 1)))
    o = o_intra + o_inter.swapaxes(0, 1)
    o = rmsnorm(o, g_o)
    o = o.transpose(0, 1, 3, 2, 4).reshape(bsz, seq, D_INNER).astype(h.dtype)
    return (o * jax.nn.silu(z)) @ w_out


def setup_inputs(seed: int = 0) -> dict:
    key = jax.random.key(seed)
    ks = jax.random.split(key, 20)
    nrm = lambda k, shape, s: jax.random.normal(k, shape, jnp.float32) * s
    return {
        "x": nrm(ks[0], (BATCH, SEQ, D_MODEL), 1.0),
        "c": nrm(ks[1], (BATCH, D_MODEL), 1.0),
        "w_ada": nrm(ks[2], (DEPTH, D_MODEL, 3 * D_MODEL), D_MODEL ** -0.5),
        "b_ada": nrm(ks[3], (DEPTH, 3 * D_MODEL), 0.02),
        "g_norm": 1.0 + nrm(ks[4], (DEPTH, D_MODEL), 0.02),
        "a_w_in": nrm(ks[5], (N_A, D_MODEL, A_IN), D_MODEL ** -0.5),
        "a_w_s": nrm(ks[6], (N_A, SGU_GROUPS, SGU_CHUNK, SGU_CHUNK), SGU_CHUNK ** -0.5),
        "a_b_s": 1.0 + nrm(ks[7], (N_A, SGU_GROUPS, SGU_CHUNK), 0.02),
        "a_g_v": 1.0 + nrm(ks[8], (N_A, D_INNER), 0.02),
        "a_w_out": nrm(ks[9], (N_A, D_INNER, D_MODEL), D_INNER ** -0.5),
        "b_w_in": nrm(ks[10], (N_B, D_MODEL, B_IN), D_MODEL ** -0.5),
        "b_w_gate_up": nrm(ks[11], (N_B, GLA_GATE_RANK, GLA_DK_TOTAL), GLA_GATE_RANK ** -0.5),
        "b_b_gate": nrm(ks[12], (N_B, GLA_DK_TOTAL), 0.1),
        "b_g_o": 1.0 + nrm(ks[13], (N_B, GLA_DV), 0.02),
        "b_w_out": nrm(ks[14], (N_B, D_INNER, D_MODEL), D_INNER ** -0.5),
        "g_final": 1.0 + nrm(ks[15], (D_MODEL,), 0.02),
    }


def reference(x, c, w_ada, b_ada, g_norm, a_w_in, a_w_s, a_b_s, a_g_v, a_w_out,
              b_w_in, b_w_gate_up, b_b_gate, b_g_o, b_w_out, g_final):
    cond = jax.nn.silu(c)
    for layer in range(DEPTH):
        mod = (cond @ w_ada[layer] + b_ada[layer])[:, None, :]
        shift, scale, gate = jnp.split(mod, 3, axis=-1)
        h = rmsnorm(x, g_norm[layer]) * (1 + scale) + shift
        j = layer // N_MIXERS
        if layer % N_MIXERS == 0:
            y = sgu_branch(h, a_w_in[j], a_w_s[j], a_b_s[j], a_g_v[j], a_w_out[j])
        else:
            y = gla_branch(h, b_w_in[j], b_w_gate_up[j], b_b_gate[j], b_g_o[j], b_w_out[j])
        x = x + gate * y
    return rmsnorm(x, g_final)
```

```python
import numpy as np
from contextlib import ExitStack
import concourse.bass as bass
import concourse.mybir as mybir
from concourse.bass_utils import run_bass_kernel_spmd

F32 = mybir.dt.float32
BF16 = mybir.dt.bfloat16
AF = mybir.ActivationFunctionType
ALU = mybir.AluOpType

S = 2048; D = 1024; DI = 2048; KC = 8; CC = 16
T = 512; NB = S // T; TC = T // 128
DEPTH = 4
EPS = 1e-6
NSLOT = 4
NCORES = 8
A_IN = 6144; B_IN = 5136

C_C = 0
C_BADA = 8
C_GN = 104
C_GV = 136
C_BG = 168
C_GO = 176
C_GF = 184
NCOLS = 192


class Buf:
    __slots__ = ("name", "w", "r")

    def __init__(self, name):
        self.name = name
        self.w = None
        self.r = {}


class Prog:
    def __init__(self, nc, es):
        self.nc = nc
        self.es = es
        self.engs = ("pe", "act", "dve", "pool", "sp")
        self.items = {e: [] for e in self.engs}
        self.sems = {}
        self.cnt = {}
        for e in ("pe", "act", "dve", "pool"):
            self.sems[e] = es.enter_context(nc.semaphore("s_" + e))
            self.cnt[e] = 0
        self.waited = {e: {} for e in self.engs}

    def newsem(self, key):
        self.sems[key] = self.es.enter_context(self.nc.semaphore("s_" + key))
        self.cnt[key] = 0
        return key

    def _deps(self, eng, reads, writes):
        need = {}

        def add(tok):
            k, v = tok
            if need.get(k, 0) < v:
                need[k] = v

        for b in reads:
            if b.w is not None:
                add(b.w)
        for b in writes:
            for k, v in b.r.items():
                add((k, v))
            if b.w is not None:
                add(b.w)
        waits = []
        for k, v in need.items():
            if k == "pe" and eng == "pe":
                continue
            if self.waited[eng].get(k, 0) < v:
                waits.append((k, v))
                self.waited[eng][k] = v
        return waits

    def _mark(self, tok, reads, writes):
        k, v = tok
        for b in reads:
            if b.r.get(k, 0) < v:
                b.r[k] = v
        for b in writes:
            b.w = tok
            b.r = {}

    def op(self, eng, fn, reads=(), writes=(), inc=True):
        waits = self._deps(eng, reads, writes)
        if inc:
            self.cnt[eng] += 1
            tok = (eng, self.cnt[eng])
        else:
            tok = (eng, self.cnt[eng] + 1)
        self.items[eng].append((waits, fn, (eng, 1) if inc else None))
        self._mark(tok, reads, writes)
        return tok

    def dma(self, eng, semkey, out, in_, reads=(), writes=()):
        waits = self._deps(eng, reads, writes)
        self.cnt[semkey] += 16
        tok = (semkey, self.cnt[semkey])
        self.items[eng].append((waits, lambda e: e.dma_start(out=out, in_=in_), (semkey, 16)))
        self._mark(tok, reads, writes)
        return tok

    def final_wait(self, eng, toks):
        waits = []
        for k, v in toks:
            waits.append((k, v))
        self.items[eng].append((waits, None, None))

    def replay(self, block):
        sems = self.sems

        def run(e, items):
            for waits, fn, inc in items:
                for k, v in waits:
                    e.wait_ge(sems[k], v)
                if fn is None:
                    continue
                ins = fn(e)
                if inc is not None:
                    ins.then_inc(sems[inc[0]], inc[1])

        @block.tensor
        def _(e):
            run(e, self.items["pe"])

        @block.scalar
        def _(e):
            run(e, self.items["act"])

        @block.vector
        def _(e):
            run(e, self.items["dve"])

        @block.gpsimd
        def _(e):
            run(e, self.items["pool"])

        @block.sync
        def _(e):
            run(e, self.items["sp"])


def build(layers, do_final):
    nc = bass.Bass("TRN2", target_bir_lowering=False)
    dram = {}

    def din(name, shape):
        dram[name] = nc.dram_tensor(name, list(shape), F32, kind="ExternalInput").ap()
        return dram[name]

    xT_d = din("xT", [D, S])
    cols_d = din("cols", [128, NCOLS])
    tri_d = din("tri", [128, 128])
    ident_d = din("ident", [128, 128])
    w_ada_d = din("w_ada", [DEPTH, D, 3 * D])
    a_w_in_d = din("a_w_in", [2, D, A_IN])
    a_w_out_d = din("a_w_out", [2, DI, D])
    b_w_in_d = din("b_w_in", [2, D, B_IN])
    b_w_out_d = din("b_w_out", [2, DI, D])
    a_wsT_d = din("a_wsT", [128, 2 * 8 * 128])
    a_bs_d = din("a_bs", [1, 2 * 8 * 128])
    b_wgu_d = din("b_wgu", [2, 16, 512])
    b_wA_d = din("b_wA", [128, 2 * KC * 16])
    outT_d = nc.dram_tensor("outT", [D, S], F32, kind="ExternalOutput").ap()

    with ExitStack() as es:
        P = Prog(nc, es)

        def sb(name, shape, dt):
            return es.enter_context(nc.sbuf_tensor("sb_" + name, list(shape), dt))

        xT = sb("xT_sb", [128, KC, S], F32)
        xT_b = [[Buf(f"xT{k}_{b}") for b in range(NB)] for k in range(KC)]
        ring = sb("ring", [128, NSLOT, 4096], BF16)
        ring_b = [Buf(f"ring{i}") for i in range(NSLOT)]
        ring_sem = [P.newsem(f"ring{i}") for i in range(NSLOT)]
        hT = sb("hT", [128, KC, T], BF16)
        hT_b = [Buf(f"hT{k}") for k in range(KC)]
        xsq = sb("xsq", [128, 2, T], BF16)
        xsq_b = [Buf("xsq0"), Buf("xsq1")]
        rbc = sb("rbc", [128, T], F32)
        rbc_b = Buf("rbc")
        tmpf = sb("tmpf", [128, 4, T], F32)
        tmpf_b = [Buf(f"tmpf{i}") for i in range(4)]
        S1 = sb("S1", [128, TC, DI], BF16)
        S1_b = [Buf(f"S1_{i}") for i in range(TC)]
        yT = sb("yT", [128, CC, T], BF16)
        yT_b = [Buf(f"yT{i}") for i in range(CC)]
        g8 = sb("g8", [128, 8, T], BF16)
        g8_b = [Buf(f"g8_{i}") for i in range(8)]
        szt = sb("szt", [128, 2, T], BF16)
        szt_b = [Buf("szt0"), Buf("szt1")]
        ksT = sb("ksT", [128, 2, T], BF16)
        ksT_b = [Buf("ksT0"), Buf("ksT1")]
        small = sb("small", [128, 64], F32)
        stat_b = [Buf(f"stat{i}") for i in range(64)]
        cols = sb("cols", [128, NCOLS], F32)
        cols_b = Buf("cols")
        condb = sb("condb", [128, KC], BF16)
        condb_b = Buf("condb")
        modc = sb("modc", [128, DEPTH, 24], F32)
        modc_b = [Buf(f"modc{l}") for l in range(DEPTH)]
        gs = sb("gs", [128, DEPTH, KC], F32)
        gs_b = [Buf(f"gs{l}") for l in range(DEPTH)]
        ones = sb("ones", [128, 128], BF16)
        ones_b = Buf("ones")
        tri = sb("tri_sb", [128, 128], F32)
        tri_b = Buf("tri")
        ident = sb("ident_sb", [128, 128], BF16)
        ident_b = Buf("ident")
        wsm = sb("wsm", [128, 2, 8, 128], BF16)
        wsm_b = Buf("wsm")
        bbc = sb("bbc", [128, 8, 128], F32)
        bbc_b = Buf("bbc")
        wgu = sb("wgu", [16, 2, 512], BF16)
        wgu_b = Buf("wgu")
        wA = sb("wA", [128, 2, KC, 16], BF16)
        wA_b = Buf("wA")
        smask = sb("smask", [128, T], BF16)
        smask_b = Buf("smask")
        negbg = sb("negbg", [128, 8], F32)
        negbg_b = Buf("negbg")
        constc = sb("constc", [128, 4], F32)
        constc_b = Buf("constc")
        alr = sb("alr", [16, T], BF16)
        alr_b = Buf("alr")
        ef2 = sb("ef2", [128, 2, T], F32)
        ef2_b = [Buf("ef0"), Buf("ef1")]
        cum2 = sb("cum2", [128, 2, T], F32)
        cum2_b = [Buf("cum0"), Buf("cum1")]
        dec = sb("dec", [128, 16], F32)
        dec_b = [Buf(f"dec{i}") for i in range(4)]
        kstok = sb("kstok", [128, 4, TC, 128], BF16)
        kstok_b = [Buf(f"kstok{i}") for i in range(4)]
        onrm = sb("onrm", [128, 2, DI], BF16)
        onrm_b = [[Buf(f"on{i}_{h}") for h in range(4)] for i in range(2)]
        state = sb("state", [128, 4, 512], F32)
        state_b = [Buf(f"state{i}") for i in range(4)]
        stbf = sb("stbf", [128, 4, 512], BF16)
        stbf_b = [Buf(f"stbf{i}") for i in range(4)]

        banks = [es.enter_context(nc.psum_tensor(f"bank{i}", [128, 512], F32)) for i in range(8)]
        bank_b = [Buf(f"bank{i}") for i in range(8)]
        bctr = [0]

        def nextbank():
            i = bctr[0] % 8
            bctr[0] += 1
            return banks[i], bank_b[i]

        misc_sem = P.newsem("misc")
        x_sem = [P.newsem(f"xld{k}") for k in range(NB)]
        wa_sem = P.newsem("wald")
        bbc_sem = P.newsem("bbcld")
        out_sem = [P.newsem(f"outst{i}") for i in range(4)]

        def mm(out, lhsT, rhs, start, stop, reads, writes, inc):
            return P.op("pe", lambda e: e.matmul(out, lhsT, rhs, start=start, stop=stop),
                        reads=reads, writes=writes, inc=inc)

        def act(out, in_, func, reads, writes, bias=None, scale=None, accum_out=None):
            kw = {}
            if bias is not None:
                kw["bias"] = bias
            if scale is not None:
                kw["scale"] = scale
            if accum_out is not None:
                kw["accum_out"] = accum_out
            return P.op("act", lambda e: e.activation(out=out, in_=in_, func=func, **kw),
                        reads=reads, writes=writes)

        def dve(fn, reads, writes):
            return P.op("dve", fn, reads=reads, writes=writes)

        piece_ctr = [0]

        def load_piece(src_ap, nk):
            s = piece_ctr[0] % NSLOT
            piece_ctr[0] += 1
            dst = ring[:, s, :].rearrange("p (k n) -> p k n", k=nk)
            P.dma("pool", ring_sem[s], dst, src_ap, reads=(), writes=(ring_b[s],))
            return s

        def rsqrt_act(out, in_, n, reads, writes):
            act(out, in_, AF.Ln, tuple(reads) + (constc_b,), writes, bias=constc[:, 0:1], scale=1.0 / n)
            act(out, out, AF.Exp, writes, writes, scale=-0.5)

        def rsqrt_inplace(ap, n, buf):
            rsqrt_act(ap, ap, n, (buf,), (buf,))

        P.dma("sp", misc_sem, cols[:, :], cols_d[:, :], writes=(cols_b,))
        P.dma("sp", misc_sem, tri[:, :], tri_d[:, :], writes=(tri_b,))
        P.dma("sp", misc_sem, rbc[:, 0:128], ident_d[:, :], writes=(rbc_b,))
        wsT = tmpf[:].rearrange("p a t -> p (a t)")
        P.dma("sp", misc_sem, wsT, a_wsT_d[:, :], writes=tuple(tmpf_b))
        P.dma("pool", wa_sem, wgu[:], b_wgu_d.rearrange("a r n -> r a n"), writes=(wgu_b,))
        P.dma("pool", wa_sem, wA[:].rearrange("p a k n -> p (a k n)"), b_wA_d[:, :], writes=(wA_b,))
        for b_ in [cols_b, tri_b, rbc_b] + tmpf_b:
            b_.w = (misc_sem, P.cnt[misc_sem])
        for b_ in (wA_b, wgu_b):
            b_.w = (wa_sem, P.cnt[wa_sem])
        def load_x_block(b, after=()):
            for k in range(KC):
                P.dma("sp", x_sem[b], xT[:, k, b * T:(b + 1) * T], xT_d[k * 128:(k + 1) * 128, b * T:(b + 1) * T],
                      reads=after, writes=(xT_b[k][b],))
            for k in range(KC):
                xT_b[k][b].w = (x_sem[b], P.cnt[x_sem[b]])

        load_x_block(0)

        dve(lambda e: e.memset(ones[:, :], 1.0), (), (ones_b,))
        dve(lambda e: e.memset(constc[:, 0:1], EPS), (), (constc_b,))
        dve(lambda e: e.memset(constc[:, 1:2], 1.0), (), (constc_b,))
        dve(lambda e: e.memset(smask[:, :], 1.0), (), (smask_b,))
        for c in range(TC):
            dve(lambda e, c=c: e.memset(smask[:, c * 128:c * 128 + 1], 0.0), (), (smask_b,))
        dve(lambda e: e.tensor_copy(ident[:, :], rbc[:, 0:128]), (rbc_b,), (ident_b,))
        wsT4 = tmpf[:].rearrange("p a t -> p (a t)").rearrange("p (a g t) -> p a g t", a=2, g=8)
        for j in range(2):
            for g in range(8):
                dve(lambda e, j=j, g=g: e.tensor_tensor(wsm[:, j, g, :], wsT4[:, j, g, :], tri[:, :], ALU.mult),
                    tuple(tmpf_b) + (tri_b,), (wsm_b,))
        dve(lambda e: e.tensor_scalar(negbg[:, :], cols[:, C_BG:C_BG + 8], -1.0, None, ALU.mult),
            (cols_b,), (negbg_b,))
        act(condb[:, :], cols[:, C_C:C_C + 8], AF.Silu, (cols_b,), (condb_b,))

        def mod_step(l, pc):
            bank, bb = nextbank()
            s = load_piece(w_ada_d[l][:, pc * 512:(pc + 1) * 512].rearrange("(k p) n -> p k n", p=128), KC)
            for j in range(4):
                for k in range(KC):
                    last = (j == 3 and k == KC - 1)
                    mm(bank[:, j:j + 1], ring[:, s, k * 512 + j * 128:k * 512 + (j + 1) * 128],
                       condb[:, k:k + 1], k == 0, k == KC - 1, (ring_b[s], condb_b), (bb,), last)
            c0 = pc * 4
            dve(lambda e: e.tensor_tensor(modc[:, l, c0:c0 + 4], bank[:, 0:4],
                                          cols[:, C_BADA + l * 24 + c0:C_BADA + l * 24 + c0 + 4], ALU.add),
                (bb, cols_b), (modc_b[l],))
            if pc == 3:
                dve(lambda e: e.scalar_tensor_tensor(gs[:, l, :], modc[:, l, 8:16], 1.0,
                                                     cols[:, C_GN + l * 8:C_GN + (l + 1) * 8], ALU.add, ALU.mult),
                    (modc_b[l], cols_b), (gs_b[l],))

        def emit_mod(l):
            for pc in range(6):
                mod_step(l, pc)

        def sumsq_bc(blk):
            bank, bb = nextbank()
            for k in range(KC):
                i = k % 2
                dve(lambda e, k=k, i=i: e.tensor_tensor(xsq[:, i, :], xT[:, k, blk * T:(blk + 1) * T],
                                                        xT[:, k, blk * T:(blk + 1) * T], ALU.mult),
                    (xT_b[k][blk],), (xsq_b[i],))
                mm(bank[:, :], ones[:, :], xsq[:, i, :], k == 0, k == KC - 1,
                   (ones_b, xsq_b[i]), (bb,), True)
            rsqrt_act(rbc[:, :], bank[:, :], D, (bb,), (rbc_b,))

        tctr = [0]

        def norm_h(l, blk):
            sumsq_bc(blk)
            for k in range(KC):
                i = tctr[0] % 4
                tctr[0] += 1
                dve(lambda e, k=k, i=i: e.scalar_tensor_tensor(
                    tmpf[:, i, :], xT[:, k, blk * T:(blk + 1) * T], gs[:, l, k:k + 1], rbc[:, :],
                    ALU.mult, ALU.mult), (xT_b[k][blk], gs_b[l], rbc_b), (tmpf_b[i],))
                act(hT[:, k, :], tmpf[:, i, :], AF.Identity, (tmpf_b[i], modc_b[l]), (hT_b[k],),
                    bias=modc[:, l, k:k + 1])

        def out_proj(l, blk, wo):
            for po in range(4):
                s = load_piece(wo.rearrange("(c p) d -> p c d", p=128)[:, :, po * 256:(po + 1) * 256], CC)
                for dd in range(2):
                    dk = po * 2 + dd
                    bank, bb = nextbank()
                    for cc in range(CC):
                        mm(bank[:, :], ring[:, s, cc * 256 + dd * 128:cc * 256 + (dd + 1) * 128], yT[:, cc, :],
                           cc == 0, cc == CC - 1, (ring_b[s], yT_b[cc]), (bb,), cc == CC - 1)
                    xs = xT[:, dk, blk * T:(blk + 1) * T]
                    dve(lambda e, xs=xs, bank=bank, dk=dk: e.scalar_tensor_tensor(
                        xs, bank[:, :], modc[:, l, 16 + dk:17 + dk], xs, ALU.mult, ALU.add),
                        (bb, modc_b[l], xT_b[dk][blk]), (xT_b[dk][blk],))

        def layer_a_block(l, blk):
            j = l // 2
            w_in = a_w_in_d[j]
            for pv in range(4):
                c0 = DI + pv * 512
                s = load_piece(w_in[:, c0:c0 + 512].rearrange("(k p) n -> p k n", p=128), KC)
                for tc in range(TC):
                    bank, bb = nextbank()
                    for k in range(KC):
                        mm(bank[:, :], hT[:, k, tc * 128:(tc + 1) * 128], ring[:, s, k * 512:(k + 1) * 512],
                           k == 0, k == KC - 1, (hT_b[k], ring_b[s]), (bb,), k == KC - 1)
                    act(S1[:, tc, pv * 512:(pv + 1) * 512], bank[:, :], AF.Gelu_apprx_tanh, (bb,), (S1_b[tc],))
            junk = onrm[:, 0, :]
            for tc in range(TC):
                act(junk, S1[:, tc, :], AF.Square, (S1_b[tc],), tuple(onrm_b[0]) + (stat_b[tc],),
                    accum_out=small[:, tc:tc + 1])
            for tc in range(TC):
                rsqrt_inplace(small[:, tc:tc + 1], DI, stat_b[tc])
                dve(lambda e, tc=tc: e.tensor_scalar(S1[:, tc, :], S1[:, tc, :], small[:, tc:tc + 1], None,
                                                     ALU.mult), (S1_b[tc], stat_b[tc]), (S1_b[tc],))
            zi = 0
            for half in range(2):
                for pu in range(2):
                    c0 = half * 1024 + pu * 512
                    s = load_piece(w_in[:, c0:c0 + 512].rearrange("(k p) n -> p k n", p=128), KC)
                    for c in range(4):
                        bank, bb = nextbank()
                        for k in range(KC):
                            mm(bank[:, :], ring[:, s, k * 512 + c * 128:k * 512 + (c + 1) * 128], hT[:, k, :],
                               k == 0, k == KC - 1, (ring_b[s], hT_b[k]), (bb,), k == KC - 1)
                        act(g8[:, pu * 4 + c, :], bank[:, :], AF.Gelu_apprx_tanh, (bb,), (g8_b[pu * 4 + c],))
                for pz in range(2):
                    c0 = 2 * DI + half * 1024 + pz * 512
                    s = load_piece(w_in[:, c0:c0 + 512].rearrange("(k p) n -> p k n", p=128), KC)
                    for c in range(4):
                        cc = half * 8 + pz * 4 + c
                        g = cc // 2
                        bank, bb = nextbank()
                        for k in range(KC):
                            mm(bank[:, :], ring[:, s, k * 512 + c * 128:k * 512 + (c + 1) * 128], hT[:, k, :],
                               k == 0, k == KC - 1, (ring_b[s], hT_b[k]), (bb,), k == KC - 1)
                        zz = zi % 2
                        zi += 1
                        act(szt[:, zz, :], bank[:, :], AF.Silu, (bb,), (szt_b[zz],))
                        bankm, bbm = nextbank()
                        for tc in range(TC):
                            mm(bankm[:, tc * 128:(tc + 1) * 128], S1[:, tc, cc * 128:(cc + 1) * 128],
                               wsm[:, j, g, :], True, True, (S1_b[tc], wsm_b), (bbm,), tc == TC - 1)
                        i = tctr[0] % 4
                        tctr[0] += 1
                        t1 = tmpf[:, i, :]
                        dve(lambda e, t1=t1, bankm=bankm, cc=cc, g=g: e.scalar_tensor_tensor(
                            t1.rearrange("p (a b) -> p a b", a=TC),
                            bankm[:, :].rearrange("p (a b) -> p a b", a=TC),
                            cols[:, C_GV + j * 16 + cc:C_GV + j * 16 + cc + 1],
                            bbc[:, g:g + 1, :].broadcast_to([128, TC, 128]), ALU.mult, ALU.add),
                            (bbm, cols_b, bbc_b), (tmpf_b[i],))
                        dve(lambda e, t1=t1, cc=cc: e.tensor_tensor(t1, t1, g8[:, cc % 8, :], ALU.mult),
                            (tmpf_b[i], g8_b[cc % 8]), (tmpf_b[i],))
                        dve(lambda e, t1=t1, cc=cc, zz=zz: e.tensor_tensor(yT[:, cc, :], t1, szt[:, zz, :], ALU.mult),
                            (tmpf_b[i], szt_b[zz]), (yT_b[cc],))

        def layer_b_block(l, blk):
            j = l // 2
            w_in = b_w_in_d[j]
            bank, bb = nextbank()
            for k in range(KC):
                mm(bank[0:16, :], wA[:, j, k, :], hT[:, k, :], k == 0, k == KC - 1, (wA_b, hT_b[k]), (bb,),
                   k == KC - 1)
            dve(lambda e, bank=bank: e.tensor_copy(alr[:, :], bank[0:16, :]), (bb,), (alr_b,))
            sq = load_piece(w_in[:, 0:512].rearrange("(k p) n -> p k n", p=128), KC)
            sk = load_piece(w_in[:, 512:1024].rearrange("(k p) n -> p k n", p=128), KC)
            def ks_transposes(hh):
                i = hh % 2
                bank, bb = nextbank()
                bv = bank[:, :].bitcast(BF16)
                for c in range(TC):
                    P.op("pe", lambda e, bv=bv, i=i, c=c: e.transpose(
                        bv[:, c * 128:(c + 1) * 128], ksT[:, i, c * 128:(c + 1) * 128], ident[:, :]),
                        reads=(ksT_b[i], ident_b), writes=(bb,), inc=(c == TC - 1))
                act(kstok[:, hh, :, :].rearrange("p c k -> p (c k)"), bv[:, 0:512], AF.Copy, (bb,), (kstok_b[hh],))

            def gate_chain(hh):
                i = hh % 2
                efh = ef2[:, i, :]
                cmh = cum2[:, i, :]
                bank, bb = nextbank()
                mm(bank[:, :], wgu[:, j, hh * 128:(hh + 1) * 128], alr[:, :], True, True, (wgu_b, alr_b), (bb,), True)
                act(efh, bank[:, :], AF.Exp, (bb, negbg_b), (ef2_b[i],),
                    bias=negbg[:, j * 4 + hh:j * 4 + hh + 1], scale=-1.0)
                act(efh, efh, AF.Ln, (ef2_b[i], constc_b), (ef2_b[i],), bias=constc[:, 1:2])
                dve(lambda e, efh=efh, cmh=cmh: e.tensor_tensor_scan(
                    out=cmh, data0=smask[:, :], data1=efh, initial=0.0, op0=ALU.mult, op1=ALU.add),
                    (smask_b, ef2_b[i]), (cum2_b[i],))
                act(efh, cmh, AF.Exp, (cum2_b[i],), (ef2_b[i],), scale=-1.0 / 16.0)
                dve(lambda e, efh=efh, hh=hh: e.tensor_copy(dec[:, hh * 4:(hh + 1) * 4], efh[:, 127::128]),
                    (ef2_b[i],), (dec_b[hh],))
                act(cmh, cmh, AF.Exp, (cum2_b[i],), (cum2_b[i],), scale=1.0 / 16.0)

            def q_mm(hh):
                bank, bb = nextbank()
                for k in range(KC):
                    mm(bank[:, :], ring[:, sq, k * 512 + hh * 128:k * 512 + (hh + 1) * 128], hT[:, k, :],
                       k == 0, k == KC - 1, (ring_b[sq], hT_b[k]), (bb,), k == KC - 1)
                return bank, bb

            def q_evac(hh, bank, bb):
                i = hh % 2
                efh = ef2[:, i, :]
                dve(lambda e, bank=bank, hh=hh, efh=efh: e.scalar_tensor_tensor(
                    g8[:, hh, :], bank[:, :], float(128 ** -0.5), efh, ALU.mult, ALU.mult),
                    (bb, ef2_b[i]), (g8_b[hh],))

            def k_part(hh):
                i = hh % 2
                cmh = cum2[:, i, :]
                bank, bb = nextbank()
                for k in range(KC):
                    mm(bank[:, :], ring[:, sk, k * 512 + hh * 128:k * 512 + (hh + 1) * 128], hT[:, k, :],
                       k == 0, k == KC - 1, (ring_b[sk], hT_b[k]), (bb,), k == KC - 1)
                dve(lambda e, bank=bank, cmh=cmh: e.tensor_tensor(cmh, bank[:, :], cmh, ALU.mult),
                    (bb, cum2_b[i]), (cum2_b[i],))
                act(g8[:, 4 + hh, :], cmh, AF.Copy, (cum2_b[i],), (g8_b[4 + hh],))
                for c in range(TC):
                    dve(lambda e, i=i, hh=hh, c=c, cmh=cmh: e.tensor_scalar(
                        ksT[:, i, c * 128:(c + 1) * 128], cmh[:, c * 128:(c + 1) * 128],
                        dec[:, hh * 4 + c:hh * 4 + c + 1], None, ALU.mult),
                        (cum2_b[i], dec_b[hh]), (ksT_b[i],))

            bq = q_mm(0)
            gate_chain(0)
            q_evac(0, *bq)
            gate_chain(1)
            k_part(0)
            for hh in range(1, 4):
                bq = q_mm(hh)
                q_evac(hh, *bq)
                if hh + 1 < 4:
                    gate_chain(hh + 1)
                k_part(hh)
                ks_transposes(hh - 1)
            sv = []
            for pv in range(4):
                c0 = 1024 + pv * 512
                sv.append(load_piece(w_in[:, c0:c0 + 512].rearrange("(k p) n -> p k n", p=128), KC))

            def v_group(tc, pv):
                s_ = sv[pv]
                bank, bb = nextbank()
                for k in range(KC):
                    mm(bank[:, :], hT[:, k, tc * 128:(tc + 1) * 128], ring[:, s_, k * 512:(k + 1) * 512],
                       k == 0, k == KC - 1, (hT_b[k], ring_b[s_]), (bb,), k == KC - 1)
                if pv % 2 == 0:
                    act(S1[:, tc, pv * 512:(pv + 1) * 512], bank[:, :], AF.Copy, (bb,), (S1_b[tc],))
                else:
                    dve(lambda e, bank=bank, tc=tc, pv=pv: e.tensor_copy(
                        S1[:, tc, pv * 512:(pv + 1) * 512], bank[:, :]), (bb,), (S1_b[tc],))

            def o_transposes(c):
                cs = slice(c * 128, (c + 1) * 128)
                ai = c % 2
                for vc in range(4):
                    bankt, bbt = nextbank()
                    bv = bankt[:, :].bitcast(BF16)
                    for hh in range(4):
                        cc = hh * 4 + vc
                        P.op("pe", lambda e, bv=bv, hh=hh, cc=cc, ai=ai: e.transpose(
                            bv[:, hh * 128:(hh + 1) * 128], onrm[:, ai, cc * 128:(cc + 1) * 128], ident[:, :]),
                            reads=(onrm_b[ai][hh], ident_b), writes=(bbt,), inc=(hh == 3))
                    wb = [yT_b[hh * 4 + vc] for hh in range(4)]
                    dve(lambda e, bv=bv, vc=vc, cs=cs: e.tensor_scalar(
                        yT[:, vc::4, cs], bv[:, 0:512].rearrange("p (a b) -> p a b", a=4),
                        cols[:, C_GO + j * 4 + vc:C_GO + j * 4 + vc + 1], None, ALU.mult),
                        (bbt, cols_b), tuple(wb))

            for pv in range(4):
                v_group(0, pv)
            ks_transposes(3)
            for c in range(TC):
                cs = slice(c * 128, (c + 1) * 128)
                ai = c % 2
                bank, bb = nextbank()
                for hh in range(4):
                    mm(bank[:, hh * 128:(hh + 1) * 128], g8[:, 4 + hh, cs], g8[:, hh, cs], True, True,
                       (g8_b[4 + hh], g8_b[hh]), (bb,), hh == 3)
                dve(lambda e, bank=bank, ai=ai: e.tensor_tensor(
                    szt[:, ai, :].rearrange("p (a b) -> p a b", a=4),
                    bank[:, :].rearrange("p (a b) -> p a b", a=4),
                    tri[:, :].unsqueeze(1).to_broadcast([128, 4, 128]), ALU.mult),
                    (bb, tri_b), (szt_b[ai],))
                if c + 1 < TC:
                    v_group(c + 1, 0)
                    v_group(c + 1, 1)
                if c > 0:
                    o_transposes(c - 1)
                for hh in range(4):
                    vs = S1[:, c, hh * 512:(hh + 1) * 512]
                    banko, bbo = nextbank()
                    mm(banko[:, :], szt[:, ai, hh * 128:(hh + 1) * 128], vs, True, False,
                       (szt_b[ai], S1_b[c]), (bbo,), False)
                    mm(banko[:, :], g8[:, hh, cs], stbf[:, hh, :], False, True,
                       (g8_b[hh], stbf_b[hh]), (bbo,), True)
                    sc = 8 + (c * 4 + hh) % 16
                    act(rbc[:, :].bitcast(BF16)[:, 0:512], banko[:, :], AF.Square, (bbo,), (rbc_b, stat_b[sc]),
                        accum_out=small[:, sc:sc + 1])
                    rsqrt_inplace(small[:, sc:sc + 1], 512, stat_b[sc])
                    act(onrm[:, ai, hh * 512:(hh + 1) * 512], banko[:, :], AF.Identity, (bbo, stat_b[sc]),
                        (onrm_b[ai][hh],), scale=small[:, sc:sc + 1])
                    banks_, bbs = nextbank()
                    mm(banks_[:, :], kstok[:, hh, c, :], vs, True, True, (kstok_b[hh], S1_b[c]), (bbs,), True)
                    dve(lambda e, hh=hh, c=c, banks_=banks_: e.scalar_tensor_tensor(
                        state[:, hh, :], state[:, hh, :], dec[:, hh * 4 + c:hh * 4 + c + 1], banks_[:, :],
                        ALU.mult, ALU.add), (state_b[hh], dec_b[hh], bbs), (state_b[hh],))
                    act(stbf[:, hh, :], state[:, hh, :], AF.Copy, (state_b[hh],), (stbf_b[hh],))
                    if hh == 1 and c + 1 < TC:
                        v_group(c + 1, 2)
                        v_group(c + 1, 3)
            zgroups = []
            for pz in range(4):
                zgroups += [(pz, c) for c in range(4)]
            zslot = {}
            pend = []

            def z_mm(pz, c):
                if pz not in zslot:
                    c0 = 1024 + DI + pz * 512
                    zslot[pz] = load_piece(w_in[:, c0:c0 + 512].rearrange("(k p) n -> p k n", p=128), KC)
                s_ = zslot[pz]
                cc = pz * 4 + c
                bank, bb = nextbank()
                for k in range(KC):
                    mm(bank[:, :], ring[:, s_, k * 512 + c * 128:k * 512 + (c + 1) * 128], hT[:, k, :],
                       k == 0, k == KC - 1, (ring_b[s_], hT_b[k]), (bb,), k == KC - 1)
                zz = cc % 2
                pend.append((cc, zz, bank, bb))

            def z_fin():
                cc, zz, bank, bb = pend.pop(0)
                act(szt[:, zz, :], bank[:, :], AF.Silu, (bb,), (szt_b[zz],))
                dve(lambda e, cc=cc, zz=zz: e.tensor_tensor(yT[:, cc, :], yT[:, cc, :], szt[:, zz, :], ALU.mult),
                    (yT_b[cc], szt_b[zz]), (yT_b[cc],))

            z_mm(*zgroups[0])
            z_mm(*zgroups[1])
            o_transposes(TC - 1)
            z_fin()
            z_fin()
            for zg in zgroups[2:]:
                z_mm(*zg)
                z_fin()

        out_toks = []

        def final_block(blk):
            sumsq_bc(blk)
            for k in range(KC):
                i = tctr[0] % 4
                tctr[0] += 1
                dve(lambda e, k=k, i=i, blk=blk: e.scalar_tensor_tensor(
                    tmpf[:, i, :], xT[:, k, blk * T:(blk + 1) * T], cols[:, C_GF + k:C_GF + k + 1], rbc[:, :],
                    ALU.mult, ALU.mult), (xT_b[k][blk], cols_b, rbc_b), (tmpf_b[i],))
                out_toks.append(P.dma("sp", out_sem[i], outT_d[k * 128:(k + 1) * 128, blk * T:(blk + 1) * T],
                                      tmpf[:, i, :], reads=(tmpf_b[i],)))

        seq = [(li, l, blk) for li, l in enumerate(layers) for blk in range(NB)]
        if seq:
            for pc in range(4):
                mod_step(layers[0], pc)
            norm_h(seq[0][1], seq[0][2])
        for b in range(1, NB):
            load_x_block(b, after=(hT_b[KC - 1],) if seq else ())
        for idx, (li, l, blk) in enumerate(seq):
            j = l // 2
            if blk == 0 and l % 2 == 0:
                P.dma("sp", bbc_sem, bbc[:].rearrange("p g t -> p (g t)"),
                      a_bs_d[0:1, j * 1024:(j + 1) * 1024].broadcast_to([128, 1024]), writes=(bbc_b,))
            if blk == 0 and l % 2 == 1:
                for hh in range(4):
                    dve(lambda e, hh=hh: e.memset(state[:, hh, :], 0.0), (), (state_b[hh],))
                    dve(lambda e, hh=hh: e.memset(stbf[:, hh, :], 0.0), (), (stbf_b[hh],))
            if l % 2 == 0:
                layer_a_block(l, blk)
            else:
                layer_b_block(l, blk)
            if idx == 0:
                mod_step(l, 4)
                mod_step(l, 5)
            if li + 1 < len(layers) and blk < 3:
                mod_step(layers[li + 1], 2 * blk)
                mod_step(layers[li + 1], 2 * blk + 1)
            if idx + 1 < len(seq):
                norm_h(seq[idx + 1][1], seq[idx + 1][2])
            out_proj(l, blk, a_w_out_d[j] if l % 2 == 0 else b_w_out_d[j])
            if do_final and li == len(layers) - 1:
                final_block(blk)

        if do_final and not seq:
            for blk in range(NB):
                final_block(blk)
        if not do_final:
            for k in range(KC):
                for blk in range(NB):
                    out_toks.append(P.dma("sp", out_sem[blk], outT_d[k * 128:(k + 1) * 128, blk * T:(blk + 1) * T],
                                          xT[:, k, blk * T:(blk + 1) * T], reads=(xT_b[k][blk],)))
        P.final_wait("sp", [(sk_, P.cnt[sk_]) for sk_ in out_sem if P.cnt[sk_] > 0])

        block = es.enter_context(nc.Block())
        P.replay(block)
    return nc


def _col(v):
    v = np.asarray(v, np.float32)
    return np.ascontiguousarray(v.reshape(-1, 128).T)


_PROG_CACHE = {}


def _get_prog(layers, do_final):
    key = (tuple(layers), do_final)
    if key not in _PROG_CACHE:
        _PROG_CACHE[key] = build(list(layers), do_final)
    return _PROG_CACHE[key]


def _in_maps(xT_list, inp):
    f = lambda a: np.ascontiguousarray(np.asarray(a, np.float32))
    c = f(inp["c"])
    tri = np.triu(np.ones((128, 128), np.float32))
    ident = np.eye(128, dtype=np.float32)
    a_w_s = f(inp["a_w_s"])
    a_wsT = np.ascontiguousarray(a_w_s.transpose(3, 0, 1, 2).reshape(128, 2 * 8 * 128))
    a_bs = f(inp["a_b_s"]).reshape(1, 2 * 8 * 128)
    shared = {
        "tri": tri, "ident": ident,
        "w_ada": f(inp["w_ada"]), "a_w_in": f(inp["a_w_in"]), "a_w_out": f(inp["a_w_out"]),
        "b_w_in": f(inp["b_w_in"]), "b_w_out": f(inp["b_w_out"]),
        "a_wsT": a_wsT, "a_bs": a_bs, "b_wgu": f(inp["b_w_gate_up"]),
        "b_wA": np.ascontiguousarray(f(inp["b_w_in"])[:, :, 5120:5136].reshape(2, KC, 128, 16)
                                     .transpose(2, 0, 1, 3).reshape(128, 2 * KC * 16)),
    }
    b_ada = f(inp["b_ada"]); g_norm = f(inp["g_norm"]); a_g_v = f(inp["a_g_v"])
    b_bg = f(inp["b_b_gate"]); b_go = f(inp["b_g_o"]); g_final = f(inp["g_final"])
    maps = []
    for b in range(NCORES):
        cols = np.zeros((128, NCOLS), np.float32)
        cols[:, C_C:C_C + 8] = _col(c[b])
        for l in range(DEPTH):
            cols[:, C_BADA + l * 24:C_BADA + (l + 1) * 24] = _col(b_ada[l])
            cols[:, C_GN + l * 8:C_GN + (l + 1) * 8] = _col(g_norm[l])
        for j in range(2):
            cols[:, C_GV + j * 16:C_GV + (j + 1) * 16] = _col(a_g_v[j])
            cols[:, C_BG + j * 4:C_BG + (j + 1) * 4] = _col(b_bg[j])
            cols[:, C_GO + j * 4:C_GO + (j + 1) * 4] = _col(b_go[j])
        cols[:, C_GF:C_GF + 8] = _col(g_final)
        m = dict(shared)
        m["xT"] = xT_list[b]
        m["cols"] = cols
        maps.append(m)
    return maps


LAUNCH_PLAN = [([0, 1, 2, 3], True)]


def kernel(**inputs):
    x = np.asarray(inputs["x"], np.float32)
    xT = [np.ascontiguousarray(x[b].T) for b in range(NCORES)]
    for layers, do_final in LAUNCH_PLAN:
        nc = _get_prog(layers, do_final)
        res = run_bass_kernel_spmd(nc, _in_maps(xT, inputs), core_ids=list(range(NCORES)))
        xT = [np.asarray(res.results[b]["outT"]) for b in range(NCORES)]
    out = np.stack([xT[b].T for b in range(NCORES)], axis=0)
    return np.ascontiguousarray(out.astype(np.float32))
```

```python
import numpy as np
from contextlib import ExitStack
import concourse.bass as bass
import concourse.mybir as mybir
from concourse.bass_utils import run_bass_kernel_spmd

F32 = mybir.dt.float32
BF16 = mybir.dt.bfloat16
AF = mybir.ActivationFunctionType
ALU = mybir.AluOpType

S = 2048; D = 1024; DI = 2048; KC = 8; CC = 16
T = 512; NB = S // T; TC = T // 128
DEPTH = 4
EPS = 1e-6
NSLOT = 4
NCORES = 8
A_IN = 6144; B_IN = 5136

C_C = 0
C_BADA = 8
C_GN = 104
C_GV = 136
C_BG = 168
C_GO = 176
C_GF = 184
NCOLS = 192


class Buf:
    __slots__ = ("name", "w", "r")

    def __init__(self, name):
        self.name = name
        self.w = None
        self.r = {}


class Prog:
    def __init__(self, nc, es):
        self.nc = nc
        self.es = es
        self.engs = ("pe", "act", "dve", "pool", "sp")
        self.items = {e: [] for e in self.engs}
        self.sems = {}
        self.cnt = {}
        for e in ("pe", "act", "dve", "pool"):
            self.sems[e] = es.enter_context(nc.semaphore("s_" + e))
            self.cnt[e] = 0
        self.waited = {e: {} for e in self.engs}

    def newsem(self, key):
        self.sems[key] = self.es.enter_context(self.nc.semaphore("s_" + key))
        self.cnt[key] = 0
        return key

    def _deps(self, eng, reads, writes):
        need = {}

        def add(tok):
            k, v = tok
            if need.get(k, 0) < v:
                need[k] = v

        for b in reads:
            if b.w is not None:
                add(b.w)
        for b in writes:
            for k, v in b.r.items():
                add((k, v))
            if b.w is not None:
                add(b.w)
        waits = []
        for k, v in need.items():
            if k == "pe" and eng == "pe":
                continue
            if self.waited[eng].get(k, 0) < v:
                waits.append((k, v))
                self.waited[eng][k] = v
        return waits

    def _mark(self, tok, reads, writes):
        k, v = tok
        for b in reads:
            if b.r.get(k, 0) < v:
                b.r[k] = v
        for b in writes:
            b.w = tok
            b.r = {}

    def op(self, eng, fn, reads=(), writes=(), inc=True):
        waits = self._deps(eng, reads, writes)
        if inc:
            self.cnt[eng] += 1
            tok = (eng, self.cnt[eng])
        else:
            tok = (eng, self.cnt[eng] + 1)
        self.items[eng].append((waits, fn, (eng, 1) if inc else None))
        self._mark(tok, reads, writes)
        return tok

    def dma(self, eng, semkey, out, in_, reads=(), writes=()):
        waits = self._deps(eng, reads, writes)
        self.cnt[semkey] += 16
        tok = (semkey, self.cnt[semkey])
        self.items[eng].append((waits, lambda e: e.dma_start(out=out, in_=in_), (semkey, 16)))
        self._mark(tok, reads, writes)
        return tok

    def final_wait(self, eng, toks):
        waits = []
        for k, v in toks:
            waits.append((k, v))
        self.items[eng].append((waits, None, None))

    def replay(self, block):
        sems = self.sems

        def run(e, items):
            for waits, fn, inc in items:
                for k, v in waits:
                    e.wait_ge(sems[k], v)
                if fn is None:
                    continue
                ins = fn(e)
                if inc is not None:
                    ins.then_inc(sems[inc[0]], inc[1])

        @block.tensor
        def _(e):
            run(e, self.items["pe"])

        @block.scalar
        def _(e):
            run(e, self.items["act"])

        @block.vector
        def _(e):
            run(e, self.items["dve"])

        @block.gpsimd
        def _(e):
            run(e, self.items["pool"])

        @block.sync
        def _(e):
            run(e, self.items["sp"])


def build(layers, do_final):
    nc = bass.Bass("TRN2", target_bir_lowering=False)
    dram = {}

    def din(name, shape):
        dram[name] = nc.dram_tensor(name, list(shape), F32, kind="ExternalInput").ap()
        return dram[name]

    xT_d = din("xT", [D, S])
    cols_d = din("cols", [128, NCOLS])
    tri_d = din("tri", [128, 128])
    ident_d = din("ident", [128, 128])
    w_ada_d = din("w_ada", [DEPTH, D, 3 * D])
    a_w_in_d = din("a_w_in", [2, D, A_IN])
    a_w_out_d = din("a_w_out", [2, DI, D])
    b_w_in_d = din("b_w_in", [2, D, B_IN])
    b_w_out_d = din("b_w_out", [2, DI, D])
    a_wsT_d = din("a_wsT", [128, 2 * 8 * 128])
    a_bs_d = din("a_bs", [1, 2 * 8 * 128])
    b_wgu_d = din("b_wgu", [2, 16, 512])
    b_wA_d = din("b_wA", [128, 2 * KC * 16])
    outT_d = nc.dram_tensor("outT", [D, S], F32, kind="ExternalOutput").ap()

    with ExitStack() as es:
        P = Prog(nc, es)

        def sb(name, shape, dt):
            return es.enter_context(nc.sbuf_tensor("sb_" + name, list(shape), dt))

        xT = sb("xT_sb", [128, KC, S], F32)
        xT_b = [[Buf(f"xT{k}_{b}") for b in range(NB)] for k in range(KC)]
        ring = sb("ring", [128, NSLOT, 4096], BF16)
        ring_b = [Buf(f"ring{i}") for i in range(NSLOT)]
        ring_sem = [P.newsem(f"ring{i}") for i in range(NSLOT)]
        hT = sb("hT", [128, KC, T], BF16)
        hT_b = [Buf(f"hT{k}") for k in range(KC)]
        xsq = sb("xsq", [128, 2, T], BF16)
        xsq_b = [Buf("xsq0"), Buf("xsq1")]
        rbc = sb("rbc", [128, T], F32)
        rbc_b = Buf("rbc")
        tmpf = sb("tmpf", [128, 4, T], F32)
        tmpf_b = [Buf(f"tmpf{i}") for i in range(4)]
        S1 = sb("S1", [128, TC, DI], BF16)
        S1_b = [Buf(f"S1_{i}") for i in range(TC)]
        yT = sb("yT", [128, CC, T], BF16)
        yT_b = [Buf(f"yT{i}") for i in range(CC)]
        g8 = sb("g8", [128, 8, T], BF16)
        g8_b = [Buf(f"g8_{i}") for i in range(8)]
        szt = sb("szt", [128, 2, T], BF16)
        szt_b = [Buf("szt0"), Buf("szt1")]
        ksT = sb("ksT", [128, 2, T], BF16)
        ksT_b = [Buf("ksT0"), Buf("ksT1")]
        small = sb("small", [128, 64], F32)
        stat_b = [Buf(f"stat{i}") for i in range(64)]
        cols = sb("cols", [128, NCOLS], F32)
        cols_b = Buf("cols")
        condb = sb("condb", [128, KC], BF16)
        condb_b = Buf("condb")
        modc = sb("modc", [128, DEPTH, 24], F32)
        modc_b = [Buf(f"modc{l}") for l in range(DEPTH)]
        gs = sb("gs", [128, DEPTH, KC], F32)
        gs_b = [Buf(f"gs{l}") for l in range(DEPTH)]
        ones = sb("ones", [128, 128], BF16)
        ones_b = Buf("ones")
        tri = sb("tri_sb", [128, 128], F32)
        tri_b = Buf("tri")
        ident = sb("ident_sb", [128, 128], BF16)
        ident_b = Buf("ident")
        wsm = sb("wsm", [128, 2, 8, 128], BF16)
        wsm_b = Buf("wsm")
        bbc = sb("bbc", [128, 8, 128], F32)
        bbc_b = Buf("bbc")
        wgu = sb("wgu", [16, 2, 512], BF16)
        wgu_b = Buf("wgu")
        wA = sb("wA", [128, 2, KC, 16], BF16)
        wA_b = Buf("wA")
        smask = sb("smask", [128, T], BF16)
        smask_b = Buf("smask")
        negbg = sb("negbg", [128, 8], F32)
        negbg_b = Buf("negbg")
        constc = sb("constc", [128, 4], F32)
        constc_b = Buf("constc")
        alr = sb("alr", [16, T], BF16)
        alr_b = Buf("alr")
        ef2 = sb("ef2", [128, 2, T], F32)
        ef2_b = [Buf("ef0"), Buf("ef1")]
        cum2 = sb("cum2", [128, 2, T], F32)
        cum2_b = [Buf("cum0"), Buf("cum1")]
        dec = sb("dec", [128, 16], F32)
        dec_b = [Buf(f"dec{i}") for i in range(4)]
        kstok = sb("kstok", [128, 4, TC, 128], BF16)
        kstok_b = [Buf(f"kstok{i}") for i in range(4)]
        onrm = sb("onrm", [128, 2, DI], BF16)
        onrm_b = [[Buf(f"on{i}_{h}") for h in range(4)] for i in range(2)]
        state = sb("state", [128, 4, 512], F32)
        state_b = [Buf(f"state{i}") for i in range(4)]
        stbf = sb("stbf", [128, 4, 512], BF16)
        stbf_b = [Buf(f"stbf{i}") for i in range(4)]

        banks = [es.enter_context(nc.psum_tensor(f"bank{i}", [128, 512], F32)) for i in range(8)]
        bank_b = [Buf(f"bank{i}") for i in range(8)]
        bctr = [0]

        def nextbank():
            i = bctr[0] % 8
            bctr[0] += 1
            return banks[i], bank_b[i]

        misc_sem = P.newsem("misc")
        x_sem = [P.newsem(f"xld{k}") for k in range(NB)]
        wa_sem = P.newsem("wald")
        bbc_sem = P.newsem("bbcld")
        out_sem = [P.newsem(f"outst{i}") for i in range(4)]

        def mm(out, lhsT, rhs, start, stop, reads, writes, inc):
            return P.op("pe", lambda e: e.matmul(out, lhsT, rhs, start=start, stop=stop),
                        reads=reads, writes=writes, inc=inc)

        def act(out, in_, func, reads, writes, bias=None, scale=None, accum_out=None):
            kw = {}
            if bias is not None:
                kw["bias"] = bias
            if scale is not None:
                kw["scale"] = scale
            if accum_out is not None:
                kw["accum_out"] = accum_out
            return P.op("act", lambda e: e.activation(out=out, in_=in_, func=func, **kw),
                        reads=reads, writes=writes)

        def dve(fn, reads, writes):
            return P.op("dve", fn, reads=reads, writes=writes)

        piece_ctr = [0]

        def load_piece(src_ap, nk):
            s = piece_ctr[0] % NSLOT
            piece_ctr[0] += 1
            dst = ring[:, s, :].rearrange("p (k n) -> p k n", k=nk)
            P.dma("pool", ring_sem[s], dst, src_ap, reads=(), writes=(ring_b[s],))
            return s

        def rsqrt_act(out, in_, n, reads, writes):
            act(out, in_, AF.Ln, tuple(reads) + (constc_b,), writes, bias=constc[:, 0:1], scale=1.0 / n)
            act(out, out, AF.Exp, writes, writes, scale=-0.5)

        def rsqrt_inplace(ap, n, buf):
            rsqrt_act(ap, ap, n, (buf,), (buf,))

        P.dma("sp", misc_sem, cols[:, :], cols_d[:, :], writes=(cols_b,))
        P.dma("sp", misc_sem, tri[:, :], tri_d[:, :], writes=(tri_b,))
        P.dma("sp", misc_sem, rbc[:, 0:128], ident_d[:, :], writes=(rbc_b,))
        wsT = tmpf[:].rearrange("p a t -> p (a t)")
        P.dma("sp", misc_sem, wsT, a_wsT_d[:, :], writes=tuple(tmpf_b))
        P.dma("pool", wa_sem, wgu[:], b_wgu_d.rearrange("a r n -> r a n"), writes=(wgu_b,))
        P.dma("pool", wa_sem, wA[:].rearrange("p a k n -> p (a k n)"), b_wA_d[:, :], writes=(wA_b,))
        for b_ in [cols_b, tri_b, rbc_b] + tmpf_b:
            b_.w = (misc_sem, P.cnt[misc_sem])
        for b_ in (wA_b, wgu_b):
            b_.w = (wa_sem, P.cnt[wa_sem])
        def load_x_block(b, after=()):
            for k in range(KC):
                P.dma("sp", x_sem[b], xT[:, k, b * T:(b + 1) * T], xT_d[k * 128:(k + 1) * 128, b * T:(b + 1) * T],
                      reads=after, writes=(xT_b[k][b],))
            for k in range(KC):
                xT_b[k][b].w = (x_sem[b], P.cnt[x_sem[b]])

        load_x_block(0)

        dve(lambda e: e.memset(ones[:, :], 1.0), (), (ones_b,))
        dve(lambda e: e.memset(constc[:, 0:1], EPS), (), (constc_b,))
        dve(lambda e: e.memset(constc[:, 1:2], 1.0), (), (constc_b,))
        dve(lambda e: e.memset(smask[:, :], 1.0), (), (smask_b,))
        for c in range(TC):
            dve(lambda e, c=c: e.memset(smask[:, c * 128:c * 128 + 1], 0.0), (), (smask_b,))
        dve(lambda e: e.tensor_copy(ident[:, :], rbc[:, 0:128]), (rbc_b,), (ident_b,))
        wsT4 = tmpf[:].rearrange("p a t -> p (a t)").rearrange("p (a g t) -> p a g t", a=2, g=8)
        for j in range(2):
            for g in range(8):
                dve(lambda e, j=j, g=g: e.tensor_tensor(wsm[:, j, g, :], wsT4[:, j, g, :], tri[:, :], ALU.mult),
                    tuple(tmpf_b) + (tri_b,), (wsm_b,))
        dve(lambda e: e.tensor_scalar(negbg[:, :], cols[:, C_BG:C_BG + 8], -1.0, None, ALU.mult),
            (cols_b,), (negbg_b,))
        act(condb[:, :], cols[:, C_C:C_C + 8], AF.Silu, (cols_b,), (condb_b,))

        def mod_step(l, pc):
            bank, bb = nextbank()
            s = load_piece(w_ada_d[l][:, pc * 512:(pc + 1) * 512].rearrange("(k p) n -> p k n", p=128), KC)
            for j in range(4):
                for k in range(KC):
                    last = (j == 3 and k == KC - 1)
                    mm(bank[:, j:j + 1], ring[:, s, k * 512 + j * 128:k * 512 + (j + 1) * 128],
                       condb[:, k:k + 1], k == 0, k == KC - 1, (ring_b[s], condb_b), (bb,), last)
            c0 = pc * 4
            dve(lambda e: e.tensor_tensor(modc[:, l, c0:c0 + 4], bank[:, 0:4],
                                          cols[:, C_BADA + l * 24 + c0:C_BADA + l * 24 + c0 + 4], ALU.add),
                (bb, cols_b), (modc_b[l],))
            if pc == 3:
                dve(lambda e: e.scalar_tensor_tensor(gs[:, l, :], modc[:, l, 8:16], 1.0,
                                                     cols[:, C_GN + l * 8:C_GN + (l + 1) * 8], ALU.add, ALU.mult),
                    (modc_b[l], cols_b), (gs_b[l],))

        def emit_mod(l):
            for pc in range(6):
                mod_step(l, pc)

        def sumsq_bc(blk):
            bank, bb = nextbank()
            for k in range(KC):
                i = k % 2
                dve(lambda e, k=k, i=i: e.tensor_tensor(xsq[:, i, :], xT[:, k, blk * T:(blk + 1) * T],
                                                        xT[:, k, blk * T:(blk + 1) * T], ALU.mult),
                    (xT_b[k][blk],), (xsq_b[i],))
                mm(bank[:, :], ones[:, :], xsq[:, i, :], k == 0, k == KC - 1,
                   (ones_b, xsq_b[i]), (bb,), True)
            rsqrt_act(rbc[:, :], bank[:, :], D, (bb,), (rbc_b,))

        tctr = [0]

        def norm_h(l, blk):
            sumsq_bc(blk)
            for k in range(KC):
                i = tctr[0] % 4
                tctr[0] += 1
                dve(lambda e, k=k, i=i: e.scalar_tensor_tensor(
                    tmpf[:, i, :], xT[:, k, blk * T:(blk + 1) * T], gs[:, l, k:k + 1], rbc[:, :],
                    ALU.mult, ALU.mult), (xT_b[k][blk], gs_b[l], rbc_b), (tmpf_b[i],))
                act(hT[:, k, :], tmpf[:, i, :], AF.Identity, (tmpf_b[i], modc_b[l]), (hT_b[k],),
                    bias=modc[:, l, k:k + 1])

        def out_proj(l, blk, wo):
            for po in range(4):
                s = load_piece(wo.rearrange("(c p) d -> p c d", p=128)[:, :, po * 256:(po + 1) * 256], CC)
                for dd in range(2):
                    dk = po * 2 + dd
                    bank, bb = nextbank()
                    for cc in range(CC):
                        mm(bank[:, :], ring[:, s, cc * 256 + dd * 128:cc * 256 + (dd + 1) * 128], yT[:, cc, :],
                           cc == 0, cc == CC - 1, (ring_b[s], yT_b[cc]), (bb,), cc == CC - 1)
                    xs = xT[:, dk, blk * T:(blk + 1) * T]
                    dve(lambda e, xs=xs, bank=bank, dk=dk: e.scalar_tensor_tensor(
                        xs, bank[:, :], modc[:, l, 16 + dk:17 + dk], xs, ALU.mult, ALU.add),
                        (bb, modc_b[l], xT_b[dk][blk]), (xT_b[dk][blk],))

        def layer_a_block(l, blk):
            j = l // 2
            w_in = a_w_in_d[j]
            for pv in range(4):
                c0 = DI + pv * 512
                s = load_piece(w_in[:, c0:c0 + 512].rearrange("(k p) n -> p k n", p=128), KC)
                for tc in range(TC):
                    bank, bb = nextbank()
                    for k in range(KC):
                        mm(bank[:, :], hT[:, k, tc * 128:(tc + 1) * 128], ring[:, s, k * 512:(k + 1) * 512],
                           k == 0, k == KC - 1, (hT_b[k], ring_b[s]), (bb,), k == KC - 1)
                    act(S1[:, tc, pv * 512:(pv + 1) * 512], bank[:, :], AF.Gelu_apprx_tanh, (bb,), (S1_b[tc],))
            junk = onrm[:, 0, :]
            for tc in range(TC):
                act(junk, S1[:, tc, :], AF.Square, (S1_b[tc],), tuple(onrm_b[0]) + (stat_b[tc],),
                    accum_out=small[:, tc:tc + 1])
            for tc in range(TC):
                rsqrt_inplace(small[:, tc:tc + 1], DI, stat_b[tc])
                dve(lambda e, tc=tc: e.tensor_scalar(S1[:, tc, :], S1[:, tc, :], small[:, tc:tc + 1], None,
                                                     ALU.mult), (S1_b[tc], stat_b[tc]), (S1_b[tc],))
            zi = 0
            for half in range(2):
                for pu in range(2):
                    c0 = half * 1024 + pu * 512
                    s = load_piece(w_in[:, c0:c0 + 512].rearrange("(k p) n -> p k n", p=128), KC)
                    for c in range(4):
                        bank, bb = nextbank()
                        for k in range(KC):
                            mm(bank[:, :], ring[:, s, k * 512 + c * 128:k * 512 + (c + 1) * 128], hT[:, k, :],
                               k == 0, k == KC - 1, (ring_b[s], hT_b[k]), (bb,), k == KC - 1)
                        act(g8[:, pu * 4 + c, :], bank[:, :], AF.Gelu_apprx_tanh, (bb,), (g8_b[pu * 4 + c],))
                for pz in range(2):
                    c0 = 2 * DI + half * 1024 + pz * 512
                    s = load_piece(w_in[:, c0:c0 + 512].rearrange("(k p) n -> p k n", p=128), KC)
                    for c in range(4):
                        cc = half * 8 + pz * 4 + c
                        g = cc // 2
                        bank, bb = nextbank()
                        for k in range(KC):
                            mm(bank[:, :], ring[:, s, k * 512 + c * 128:k * 512 + (c + 1) * 128], hT[:, k, :],
                               k == 0, k == KC - 1, (ring_b[s], hT_b[k]), (bb,), k == KC - 1)
                        zz = zi % 2
                        zi += 1
                        act(szt[:, zz, :], bank[:, :], AF.Silu, (bb,), (szt_b[zz],))
                        bankm, bbm = nextbank()
                        for tc in range(TC):
                            mm(bankm[:, tc * 128:(tc + 1) * 128], S1[:, tc, cc * 128:(cc + 1) * 128],
                               wsm[:, j, g, :], True, True, (S1_b[tc], wsm_b), (bbm,), tc == TC - 1)
                        i = tctr[0] % 4
                        tctr[0] += 1
                        t1 = tmpf[:, i, :]
                        dve(lambda e, t1=t1, bankm=bankm, cc=cc, g=g: e.scalar_tensor_tensor(
                            t1.rearrange("p (a b) -> p a b", a=TC),
                            bankm[:, :].rearrange("p (a b) -> p a b", a=TC),
                            cols[:, C_GV + j * 16 + cc:C_GV + j * 16 + cc + 1],
                            bbc[:, g:g + 1, :].broadcast_to([128, TC, 128]), ALU.mult, ALU.add),
                            (bbm, cols_b, bbc_b), (tmpf_b[i],))
                        dve(lambda e, t1=t1, cc=cc: e.tensor_tensor(t1, t1, g8[:, cc % 8, :], ALU.mult),
                            (tmpf_b[i], g8_b[cc % 8]), (tmpf_b[i],))
                        dve(lambda e, t1=t1, cc=cc, zz=zz: e.tensor_tensor(yT[:, cc, :], t1, szt[:, zz, :], ALU.mult),
                            (tmpf_b[i], szt_b[zz]), (yT_b[cc],))

        def layer_b_block(l, blk):
            j = l // 2
            w_in = b_w_in_d[j]
            bank, bb = nextbank()
            for k in range(KC):
                mm(bank[0:16, :], wA[:, j, k, :], hT[:, k, :], k == 0, k == KC - 1, (wA_b, hT_b[k]), (bb,),
                   k == KC - 1)
            dve(lambda e, bank=bank: e.tensor_copy(alr[:, :], bank[0:16, :]), (bb,), (alr_b,))
            sq = load_piece(w_in[:, 0:512].rearrange("(k p) n -> p k n", p=128), KC)
            sk = load_piece(w_in[:, 512:1024].rearrange("(k p) n -> p k n", p=128), KC)
            def ks_transposes(hh):
                i = hh % 2
                bank, bb = nextbank()
                bv = bank[:, :].bitcast(BF16)
                for c in range(TC):
                    P.op("pe", lambda e, bv=bv, i=i, c=c: e.transpose(
                        bv[:, c * 128:(c + 1) * 128], ksT[:, i, c * 128:(c + 1) * 128], ident[:, :]),
                        reads=(ksT_b[i], ident_b), writes=(bb,), inc=(c == TC - 1))
                act(kstok[:, hh, :, :].rearrange("p c k -> p (c k)"), bv[:, 0:512], AF.Copy, (bb,), (kstok_b[hh],))

            def gate_chain(hh):
                i = hh % 2
                efh = ef2[:, i, :]
                cmh = cum2[:, i, :]
                bank, bb = nextbank()
                mm(bank[:, :], wgu[:, j, hh * 128:(hh + 1) * 128], alr[:, :], True, True, (wgu_b, alr_b), (bb,), True)
                act(efh, bank[:, :], AF.Exp, (bb, negbg_b), (ef2_b[i],),
                    bias=negbg[:, j * 4 + hh:j * 4 + hh + 1], scale=-1.0)
                act(efh, efh, AF.Ln, (ef2_b[i], constc_b), (ef2_b[i],), bias=constc[:, 1:2])
                dve(lambda e, efh=efh, cmh=cmh: e.tensor_tensor_scan(
                    out=cmh, data0=smask[:, :], data1=efh, initial=0.0, op0=ALU.mult, op1=ALU.add),
                    (smask_b, ef2_b[i]), (cum2_b[i],))
                act(efh, cmh, AF.Exp, (cum2_b[i],), (ef2_b[i],), scale=-1.0 / 16.0)
                dve(lambda e, efh=efh, hh=hh: e.tensor_copy(dec[:, hh * 4:(hh + 1) * 4], efh[:, 127::128]),
                    (ef2_b[i],), (dec_b[hh],))
                act(cmh, cmh, AF.Exp, (cum2_b[i],), (cum2_b[i],), scale=1.0 / 16.0)

            def q_mm(hh):
                bank, bb = nextbank()
                for k in range(KC):
                    mm(bank[:, :], ring[:, sq, k * 512 + hh * 128:k * 512 + (hh + 1) * 128], hT[:, k, :],
                       k == 0, k == KC - 1, (ring_b[sq], hT_b[k]), (bb,), k == KC - 1)
                return bank, bb

            def q_evac(hh, bank, bb):
                i = hh % 2
                efh = ef2[:, i, :]
                dve(lambda e, bank=bank, hh=hh, efh=efh: e.scalar_tensor_tensor(
                    g8[:, hh, :], bank[:, :], float(128 ** -0.5), efh, ALU.mult, ALU.mult),
                    (bb, ef2_b[i]), (g8_b[hh],))

            def k_part(hh):
                i = hh % 2
                cmh = cum2[:, i, :]
                bank, bb = nextbank()
                for k in range(KC):
                    mm(bank[:, :], ring[:, sk, k * 512 + hh * 128:k * 512 + (hh + 1) * 128], hT[:, k, :],
                       k == 0, k == KC - 1, (ring_b[sk], hT_b[k]), (bb,), k == KC - 1)
                dve(lambda e, bank=bank, cmh=cmh: e.tensor_tensor(cmh, bank[:, :], cmh, ALU.mult),
                    (bb, cum2_b[i]), (cum2_b[i],))
                act(g8[:, 4 + hh, :], cmh, AF.Copy, (cum2_b[i],), (g8_b[4 + hh],))
                for c in range(TC):
                    dve(lambda e, i=i, hh=hh, c=c, cmh=cmh: e.tensor_scalar(
                        ksT[:, i, c * 128:(c + 1) * 128], cmh[:, c * 128:(c + 1) * 128],
                        dec[:, hh * 4 + c:hh * 4 + c + 1], None, ALU.mult),
                        (cum2_b[i], dec_b[hh]), (ksT_b[i],))

            bq = q_mm(0)
            gate_chain(0)
            q_evac(0, *bq)
            gate_chain(1)
            k_part(0)
            for hh in range(1, 4):
                bq = q_mm(hh)
                q_evac(hh, *bq)
                if hh + 1 < 4:
                    gate_chain(hh + 1)
                k_part(hh)
                ks_transposes(hh - 1)
            sv = []
            for pv in range(4):
                c0 = 1024 + pv * 512
                sv.append(load_piece(w_in[:, c0:c0 + 512].rearrange("(k p) n -> p k n", p=128), KC))

            def v_group(tc, pv):
                s_ = sv[pv]
                bank, bb = nextbank()
                for k in range(KC):
                    mm(bank[:, :], hT[:, k, tc * 128:(tc + 1) * 128], ring[:, s_, k * 512:(k + 1) * 512],
                       k == 0, k == KC - 1, (hT_b[k], ring_b[s_]), (bb,), k == KC - 1)
                if pv % 2 == 0:
                    act(S1[:, tc, pv * 512:(pv + 1) * 512], bank[:, :], AF.Copy, (bb,), (S1_b[tc],))
                else:
                    dve(lambda e, bank=bank, tc=tc, pv=pv: e.tensor_copy(
                        S1[:, tc, pv * 512:(pv + 1) * 512], bank[:, :]), (bb,), (S1_b[tc],))

            def o_transposes(c):
                cs = slice(c * 128, (c + 1) * 128)
                ai = c % 2
                for vc in range(4):
                    bankt, bbt = nextbank()
                    bv = bankt[:, :].bitcast(BF16)
                    for hh in range(4):
                        cc = hh * 4 + vc
                        P.op("pe", lambda e, bv=bv, hh=hh, cc=cc, ai=ai: e.transpose(
                            bv[:, hh * 128:(hh + 1) * 128], onrm[:, ai, cc * 128:(cc + 1) * 128], ident[:, :]),
                            reads=(onrm_b[ai][hh], ident_b), writes=(bbt,), inc=(hh == 3))
                    wb = [yT_b[hh * 4 + vc] for hh in range(4)]
                    dve(lambda e, bv=bv, vc=vc, cs=cs: e.tensor_scalar(
                        yT[:, vc::4, cs], bv[:, 0:512].rearrange("p (a b) -> p a b", a=4),
                        cols[:, C_GO + j * 4 + vc:C_GO + j * 4 + vc + 1], None, ALU.mult),
                        (bbt, cols_b), tuple(wb))

            v_group(0, 0)
            v_group(0, 1)
            ks_transposes(3)
            for c in range(TC):
                cs = slice(c * 128, (c + 1) * 128)
                ai = c % 2
                bank, bb = nextbank()
                for hh in range(4):
                    mm(bank[:, hh * 128:(hh + 1) * 128], g8[:, 4 + hh, cs], g8[:, hh, cs], True, True,
                       (g8_b[4 + hh], g8_b[hh]), (bb,), hh == 3)
                dve(lambda e, bank=bank, ai=ai: e.tensor_tensor(
                    szt[:, ai, :].rearrange("p (a b) -> p a b", a=4),
                    bank[:, :].rearrange("p (a b) -> p a b", a=4),
                    tri[:, :].unsqueeze(1).to_broadcast([128, 4, 128]), ALU.mult),
                    (bb, tri_b), (szt_b[ai],))
                if c + 1 < TC:
                    v_group(c + 1, 0)
                    v_group(c + 1, 1)
                if c > 0:
                    o_transposes(c - 1)
                for hh in range(4):
                    vs = S1[:, c, hh * 512:(hh + 1) * 512]
                    banko, bbo = nextbank()
                    mm(banko[:, :], szt[:, ai, hh * 128:(hh + 1) * 128], vs, True, False,
                       (szt_b[ai], S1_b[c]), (bbo,), False)
                    mm(banko[:, :], g8[:, hh, cs], stbf[:, hh, :], False, True,
                       (g8_b[hh], stbf_b[hh]), (bbo,), True)
                    sc = 8 + (c * 4 + hh) % 16
                    act(rbc[:, :].bitcast(BF16)[:, 0:512], banko[:, :], AF.Square, (bbo,), (rbc_b, stat_b[sc]),
                        accum_out=small[:, sc:sc + 1])
                    rsqrt_inplace(small[:, sc:sc + 1], 512, stat_b[sc])
                    act(onrm[:, ai, hh * 512:(hh + 1) * 512], banko[:, :], AF.Identity, (bbo, stat_b[sc]),
                        (onrm_b[ai][hh],), scale=small[:, sc:sc + 1])
                    banks_, bbs = nextbank()
                    mm(banks_[:, :], kstok[:, hh, c, :], vs, True, True, (kstok_b[hh], S1_b[c]), (bbs,), True)
                    dve(lambda e, hh=hh, c=c, banks_=banks_: e.scalar_tensor_tensor(
                        state[:, hh, :], state[:, hh, :], dec[:, hh * 4 + c:hh * 4 + c + 1], banks_[:, :],
                        ALU.mult, ALU.add), (state_b[hh], dec_b[hh], bbs), (state_b[hh],))
                    dve(lambda e, hh=hh: e.tensor_copy(stbf[:, hh, :], state[:, hh, :]),
                        (state_b[hh],), (stbf_b[hh],))
                    if hh == 0 and c == 0:
                        v_group(0, 2)
                        v_group(0, 3)
                    if hh == 1 and c + 1 < TC:
                        v_group(c + 1, 2)
                        v_group(c + 1, 3)
            zgroups = []
            for pz in range(4):
                zgroups += [(pz, c) for c in range(4)]
            zslot = {}
            pend = []

            def z_mm(pz, c):
                if pz not in zslot:
                    c0 = 1024 + DI + pz * 512
                    zslot[pz] = load_piece(w_in[:, c0:c0 + 512].rearrange("(k p) n -> p k n", p=128), KC)
                s_ = zslot[pz]
                cc = pz * 4 + c
                bank, bb = nextbank()
                for k in range(KC):
                    mm(bank[:, :], ring[:, s_, k * 512 + c * 128:k * 512 + (c + 1) * 128], hT[:, k, :],
                       k == 0, k == KC - 1, (ring_b[s_], hT_b[k]), (bb,), k == KC - 1)
                zz = cc % 2
                pend.append((cc, zz, bank, bb))

            def z_fin():
                cc, zz, bank, bb = pend.pop(0)
                act(szt[:, zz, :], bank[:, :], AF.Silu, (bb,), (szt_b[zz],))
                dve(lambda e, cc=cc, zz=zz: e.tensor_tensor(yT[:, cc, :], yT[:, cc, :], szt[:, zz, :], ALU.mult),
                    (yT_b[cc], szt_b[zz]), (yT_b[cc],))

            z_mm(*zgroups[0])
            z_mm(*zgroups[1])
            o_transposes(TC - 1)
            z_fin()
            z_fin()
            for zg in zgroups[2:]:
                z_mm(*zg)
                z_fin()

        out_toks = []

        def final_block(blk):
            sumsq_bc(blk)
            for k in range(KC):
                i = tctr[0] % 4
                tctr[0] += 1
                dve(lambda e, k=k, i=i, blk=blk: e.scalar_tensor_tensor(
                    tmpf[:, i, :], xT[:, k, blk * T:(blk + 1) * T], cols[:, C_GF + k:C_GF + k + 1], rbc[:, :],
                    ALU.mult, ALU.mult), (xT_b[k][blk], cols_b, rbc_b), (tmpf_b[i],))
                out_toks.append(P.dma("sp", out_sem[i], outT_d[k * 128:(k + 1) * 128, blk * T:(blk + 1) * T],
                                      tmpf[:, i, :], reads=(tmpf_b[i],)))

        seq = [(li, l, blk) for li, l in enumerate(layers) for blk in range(NB)]
        if seq:
            for pc in range(4):
                mod_step(layers[0], pc)
            norm_h(seq[0][1], seq[0][2])
        for b in range(1, NB):
            load_x_block(b, after=(hT_b[KC - 1],) if seq else ())
        for idx, (li, l, blk) in enumerate(seq):
            j = l // 2
            if blk == 0 and l % 2 == 0:
                P.dma("sp", bbc_sem, bbc[:].rearrange("p g t -> p (g t)"),
                      a_bs_d[0:1, j * 1024:(j + 1) * 1024].broadcast_to([128, 1024]), writes=(bbc_b,))
            if blk == 0 and l % 2 == 1:
                for hh in range(4):
                    dve(lambda e, hh=hh: e.memset(state[:, hh, :], 0.0), (), (state_b[hh],))
                    dve(lambda e, hh=hh: e.memset(stbf[:, hh, :], 0.0), (), (stbf_b[hh],))
            if l % 2 == 0:
                layer_a_block(l, blk)
            else:
                layer_b_block(l, blk)
            if idx == 0:
                mod_step(l, 4)
                mod_step(l, 5)
            if li + 1 < len(layers) and blk < 3:
                mod_step(layers[li + 1], 2 * blk)
                mod_step(layers[li + 1], 2 * blk + 1)
            if idx + 1 < len(seq):
                norm_h(seq[idx + 1][1], seq[idx + 1][2])
            out_proj(l, blk, a_w_out_d[j] if l % 2 == 0 else b_w_out_d[j])
            if do_final and li == len(layers) - 1:
                final_block(blk)

        if do_final and not seq:
            for blk in range(NB):
                final_block(blk)
        if not do_final:
            for k in range(KC):
                for blk in range(NB):
                    out_toks.append(P.dma("sp", out_sem[blk], outT_d[k * 128:(k + 1) * 128, blk * T:(blk + 1) * T],
                                          xT[:, k, blk * T:(blk + 1) * T], reads=(xT_b[k][blk],)))
        P.final_wait("sp", [(sk_, P.cnt[sk_]) for sk_ in out_sem if P.cnt[sk_] > 0])

        block = es.enter_context(nc.Block())
        P.replay(block)
    return nc


def _col(v):
    v = np.asarray(v, np.float32)
    return np.ascontiguousarray(v.reshape(-1, 128).T)


_PROG_CACHE = {}


def _get_prog(layers, do_final):
    key = (tuple(layers), do_final)
    if key not in _PROG_CACHE:
        _PROG_CACHE[key] = build(list(layers), do_final)
    return _PROG_CACHE[key]


def _in_maps(xT_list, inp):
    f = lambda a: np.ascontiguousarray(np.asarray(a, np.float32))
    c = f(inp["c"])
    tri = np.triu(np.ones((128, 128), np.float32))
    ident = np.eye(128, dtype=np.float32)
    a_w_s = f(inp["a_w_s"])
    a_wsT = np.ascontiguousarray(a_w_s.transpose(3, 0, 1, 2).reshape(128, 2 * 8 * 128))
    a_bs = f(inp["a_b_s"]).reshape(1, 2 * 8 * 128)
    shared = {
        "tri": tri, "ident": ident,
        "w_ada": f(inp["w_ada"]), "a_w_in": f(inp["a_w_in"]), "a_w_out": f(inp["a_w_out"]),
        "b_w_in": f(inp["b_w_in"]), "b_w_out": f(inp["b_w_out"]),
        "a_wsT": a_wsT, "a_bs": a_bs, "b_wgu": f(inp["b_w_gate_up"]),
        "b_wA": np.ascontiguousarray(f(inp["b_w_in"])[:, :, 5120:5136].reshape(2, KC, 128, 16)
                                     .transpose(2, 0, 1, 3).reshape(128, 2 * KC * 16)),
    }
    b_ada = f(inp["b_ada"]); g_norm = f(inp["g_norm"]); a_g_v = f(inp["a_g_v"])
    b_bg = f(inp["b_b_gate"]); b_go = f(inp["b_g_o"]); g_final = f(inp["g_final"])
    maps = []
    for b in range(NCORES):
        cols = np.zeros((128, NCOLS), np.float32)
        cols[:, C_C:C_C + 8] = _col(c[b])
        for l in range(DEPTH):
            cols[:, C_BADA + l * 24:C_BADA + (l + 1) * 24] = _col(b_ada[l])
            cols[:, C_GN + l * 8:C_GN + (l + 1) * 8] = _col(g_norm[l])
        for j in range(2):
            cols[:, C_GV + j * 16:C_GV + (j + 1) * 16] = _col(a_g_v[j])
            cols[:, C_BG + j * 4:C_BG + (j + 1) * 4] = _col(b_bg[j])
            cols[:, C_GO + j * 4:C_GO + (j + 1) * 4] = _col(b_go[j])
        cols[:, C_GF:C_GF + 8] = _col(g_final)
        m = dict(shared)
        m["xT"] = xT_list[b]
        m["cols"] = cols
        maps.append(m)
    return maps


LAUNCH_PLAN = [([0, 1, 2, 3], True)]


def kernel(**inputs):
    x = np.asarray(inputs["x"], np.float32)
    xT = [np.ascontiguousarray(x[b].T) for b in range(NCORES)]
    for layers, do_final in LAUNCH_PLAN:
        nc = _get_prog(layers, do_final)
        res = run_bass_kernel_spmd(nc, _in_maps(xT, inputs), core_ids=list(range(NCORES)))
        xT = [np.asarray(res.results[b]["outT"]) for b in range(NCORES)]
    out = np.stack([xT[b].T for b in range(NCORES)], axis=0)
    return np.ascontiguousarray(out.astype(np.float32))
```
